# Optimizing a Trainium2 kernel written in Bass

```python
import math
import jax
import jax.numpy as jnp
from jax import lax
import numpy as np

D_MODEL = 1024
BATCH = 32
SEQ = 2048
DEPTH = 4

CHUNK = 64
N_BRANCH = 4
BRANCH_WIDTH = D_MODEL // N_BRANCH
RET_HEADS = 4
RET_DV = BRANCH_WIDTH // RET_HEADS
RET_DK = RET_DV // 2
ATT_HEADS = 4
ATT_DH = BRANCH_WIDTH // ATT_HEADS
ATT_LEFT_CHUNKS = 8
ATT_MAX_REL = 128
GLA_HEADS = 4
GLA_DV = BRANCH_WIDTH // GLA_HEADS
GLA_DK = GLA_DV // 2
GLA_GATE_RANK = 16
GLA_GATE_TAU = 16.0
S5_GROUP = 16
S5_GROUPS = BRANCH_WIDTH // S5_GROUP
S5_STATE = 64
D_FF = ((8 * D_MODEL // 3 + 255) // 256) * 256
DEEPNORM_ALPHA = (2.0 * DEPTH) ** 0.25
DEEPNORM_BETA = (8.0 * DEPTH) ** -0.25
LN_EPS = 1e-5
IN_SIZES = (
    RET_HEADS * RET_DK, RET_HEADS * RET_DK, RET_HEADS * RET_DV, RET_HEADS * RET_DV,
    ATT_HEADS * ATT_DH, ATT_HEADS * ATT_DH, ATT_HEADS * ATT_DH,
    GLA_HEADS * GLA_DK, GLA_HEADS * GLA_DK, GLA_HEADS * GLA_DV, GLA_HEADS * GLA_DV,
    GLA_GATE_RANK,
    BRANCH_WIDTH,
)
IN_WIDTH = sum(IN_SIZES)

kernel_name = "hybrid_gated_streaming_encoder"


def _split_cols(h, sizes):
    out, start = [], 0
    for s in sizes:
        out.append(h[..., start:start + s])
        start += s
    return out


def _layer_norm(x, g, b):
    xf = x.astype(jnp.float32)
    mu = jnp.mean(xf, axis=-1, keepdims=True)
    var = jnp.mean(jnp.square(xf - mu), axis=-1, keepdims=True)
    y = (xf - mu) * lax.rsqrt(var + LN_EPS) * g.astype(jnp.float32) + b.astype(jnp.float32)
    return y.astype(x.dtype)


def _head_norm(o):
    mu = jnp.mean(o, axis=-1, keepdims=True)
    var = jnp.mean(jnp.square(o - mu), axis=-1, keepdims=True)
    return (o - mu) * lax.rsqrt(var + LN_EPS)


def _rotary(x, pos):
    half = x.shape[-1] // 2
    inv = 1.0 / (10000.0 ** (jnp.arange(half, dtype=jnp.float32) / half))
    ang = pos.astype(jnp.float32)[:, None] * inv[None, :]
    cos = jnp.cos(ang)[None, :, None, :]
    sin = jnp.sin(ang)[None, :, None, :]
    x1, x2 = x[..., :half], x[..., half:]
    return jnp.concatenate([x1 * cos - x2 * sin, x1 * sin + x2 * cos], axis=-1)


def _retention(q, k, v, g):
    bsz, seq = q.shape[:2]
    nc = seq // CHUNK
    pos = jnp.arange(seq)
    qf = _rotary(q.astype(jnp.float32).reshape(bsz, seq, RET_HEADS, RET_DK), pos)
    kf = _rotary(k.astype(jnp.float32).reshape(bsz, seq, RET_HEADS, RET_DK), pos) * RET_DK ** -0.5
    vf = v.astype(jnp.float32)
    log_g = jnp.log(1.0 - 2.0 ** (-5.0 - jnp.arange(RET_HEADS, dtype=jnp.float32)))
    j = jnp.arange(CHUNK, dtype=jnp.float32)
    d_intra = jnp.exp(log_g[:, None, None] * jnp.abs(j[:, None] - j[None, :]))
    xi = jnp.exp(log_g[None, :] * (j[:, None] + 1.0))
    zeta = jnp.exp(log_g[None, :] * (CHUNK - 1.0 - j[:, None]))
    decay_chunk = jnp.exp(log_g * CHUNK)
    qc = qf.reshape(bsz, nc, CHUNK, RET_HEADS, RET_DK)
    kc = kf.reshape(bsz, nc, CHUNK, RET_HEADS, RET_DK)
    vc = vf.reshape(bsz, nc, CHUNK, RET_HEADS, RET_DV)
    s = jnp.einsum('bnqhd,bnkhd->bnhqk', qc, kc) * d_intra
    o_intra = jnp.einsum('bnhqk,bnkhe->bnqhe', s, vc)
    upd = jnp.einsum('bnkhd,bnkhe->bnhde', kc * zeta[None, None, :, :, None], vc)

    def step(state, u):
        return decay_chunk[None, :, None, None] * state + u, state

    init = jnp.zeros((bsz, RET_HEADS, RET_DK, RET_DV), jnp.float32)
    _, r_prev = lax.scan(step, init, jnp.moveaxis(upd, 1, 0))
    r_prev = jnp.moveaxis(r_prev, 0, 1)
    o_cross = jnp.einsum('bnqhd,bnhde->bnqhe', qc * xi[None, None, :, :, None], r_prev)
    o = _head_norm(o_intra + o_cross).reshape(bsz, seq, RET_HEADS * RET_DV)
    return (jax.nn.silu(g.astype(jnp.float32)) * o).astype(q.dtype)


def _chunk_attention(q, k, v, rel_bias):
    bsz, seq = q.shape[:2]
    nc = seq // CHUNK
    left = ATT_LEFT_CHUNKS * CHUNK
    band = left + CHUNK
    qf = q.astype(jnp.float32).reshape(bsz, seq, ATT_HEADS, ATT_DH) * ATT_DH ** -0.5
    kp = jnp.pad(k.astype(jnp.float32).reshape(bsz, seq, ATT_HEADS, ATT_DH), ((0, 0), (left, 0), (0, 0), (0, 0)))
    vp = jnp.pad(v.astype(jnp.float32).reshape(bsz, seq, ATT_HEADS, ATT_DH), ((0, 0), (left, 0), (0, 0), (0, 0)))
    jq = jnp.arange(CHUNK)[:, None]
    pk = jnp.arange(band)[None, :]
    rel = jnp.clip(left + jq - pk, -ATT_MAX_REL, ATT_MAX_REL) + ATT_MAX_REL
    bias = rel_bias.astype(jnp.float32)[:, rel]

    def one_chunk(i):
        start = i * CHUNK
        qi = lax.dynamic_slice_in_dim(qf, start, CHUNK, axis=1)
        ki = lax.dynamic_slice_in_dim(kp, start, band, axis=1)
        vi = lax.dynamic_slice_in_dim(vp, start, band, axis=1)
        s = jnp.einsum('bqhd,bkhd->bhqk', qi, ki) + bias[None]
        valid = (start - left + jnp.arange(band)) >= 0
        s = jnp.where(valid[None, None, None, :], s, -1e30)
        p = jax.nn.softmax(s, axis=-1)
        return jnp.einsum('bhqk,bkhd->bqhd', p, vi)

    o = lax.map(one_chunk, jnp.arange(nc))
    return jnp.moveaxis(o, 0, 1).reshape(bsz, seq, ATT_HEADS * ATT_DH).astype(q.dtype)


def _gla(q, k, v, r, a_lr, w_a_up, b_a):
    bsz, seq = q.shape[:2]
    nc = seq // CHUNK
    shp_k = (bsz, nc, CHUNK, GLA_HEADS, GLA_DK)
    qc = q.astype(jnp.float32).reshape(shp_k) * GLA_DK ** -0.5
    kc = k.astype(jnp.float32).reshape(shp_k)
    vc = v.astype(jnp.float32).reshape(bsz, nc, CHUNK, GLA_HEADS, GLA_DV)
    z = a_lr.astype(jnp.float32) @ w_a_up.astype(jnp.float32) + b_a.astype(jnp.float32)
    log_a = (jax.nn.log_sigmoid(z) / GLA_GATE_TAU).reshape(shp_k)
    cum = jnp.cumsum(log_a, axis=2)
    last = cum[:, :, -1:]
    k_dec = kc * jnp.exp(last - cum)
    upd = jnp.einsum('bnchk,bnchv->bnhkv', k_dec, vc)
    g_chunk = jnp.exp(last[:, :, 0])

    def combine(a, b):
        ga, ua = a
        gb, ub = b
        return ga * gb, gb[..., None] * ua + ub

    _, states = lax.associative_scan(combine, (g_chunk, upd), axis=1)
    o = jnp.einsum('bnchk,bnhkv->bnchv', qc, states)
    o = _head_norm(o).reshape(bsz, seq, GLA_HEADS * GLA_DV)
    return (jax.nn.silu(r.astype(jnp.float32)) * o).astype(q.dtype)


def _s5(u, lam_re, lam_im, log_dt, b_re, b_im, c_re, c_im, d_skip, w_glu, b_glu):
    bsz, seq = u.shape[:2]
    f32 = jnp.float32
    uf = u.astype(f32).reshape(bsz, seq, S5_GROUPS, S5_GROUP)
    lr, li = lam_re.astype(f32), lam_im.astype(f32)
    dt = jnp.exp(log_dt.astype(f32))[:, None]
    mag = jnp.exp(lr * dt)
    ab_re, ab_im = mag * jnp.cos(li * dt), mag * jnp.sin(li * dt)
    den = lr * lr + li * li
    nr, ni = ab_re - 1.0, ab_im
    coef_re = (nr * lr + ni * li) / den
    coef_im = (ni * lr - nr * li) / den
    br, bi = b_re.astype(f32), b_im.astype(f32)
    bb_re = coef_re[..., None] * br - coef_im[..., None] * bi
    bb_im = coef_re[..., None] * bi + coef_im[..., None] * br
    bu_re = jnp.einsum('bsgi,gpi->bsgp', uf, bb_re)
    bu_im = jnp.einsum('bsgi,gpi->bsgp', uf, bb_im)
    a_re = jnp.broadcast_to(ab_re, bu_re.shape)
    a_im = jnp.broadcast_to(ab_im, bu_im.shape)

    def combine(e1, e2):
        a1r, a1i, b1r, b1i = e1
        a2r, a2i, b2r, b2i = e2
        return (a2r * a1r - a2i * a1i, a2r * a1i + a2i * a1r,
                a2r * b1r - a2i * b1i + b2r, a2r * b1i + a2i * b1r + b2i)

    _, _, xr, xim = lax.associative_scan(combine, (a_re, a_im, bu_re, bu_im), axis=1)
    y = jnp.einsum('gip,bsgp->bsgi', c_re.astype(f32), xr) - jnp.einsum('gip,bsgp->bsgi', c_im.astype(f32), xim)
    y = y.reshape(bsz, seq, BRANCH_WIDTH) + d_skip.astype(f32) * uf.reshape(bsz, seq, BRANCH_WIDTH)
    y = jax.nn.gelu(y)
    y = y * jax.nn.sigmoid(y @ w_glu.astype(f32) + b_glu.astype(f32))
    return y.astype(u.dtype)


def setup_inputs(seed: int = 0) -> dict:
    key = jax.random.key(seed)
    ks = jax.random.split(key, 28)
    f32 = jnp.float32

    def nrm(k, shape, scale):
        return jax.random.normal(k, shape, f32) * scale

    n_idx = jnp.arange(S5_STATE, dtype=f32)
    gp = (DEPTH, S5_GROUPS, S5_STATE)
    return {
        'x': nrm(ks[0], (BATCH, SEQ, D_MODEL), 1.0),
        'w_in': nrm(ks[1], (DEPTH, D_MODEL, IN_WIDTH), D_MODEL ** -0.5),
        'gla_w_a': nrm(ks[2], (DEPTH, GLA_GATE_RANK, GLA_HEADS * GLA_DK), GLA_GATE_RANK ** -0.5),
        'gla_b_a': nrm(ks[3], (DEPTH, GLA_HEADS * GLA_DK), 0.1),
        'att_rel_bias': nrm(ks[4], (DEPTH, ATT_HEADS, 2 * ATT_MAX_REL + 1), 0.1),
        's5_lambda_re': -0.5 + nrm(ks[5], gp, 0.01),
        's5_lambda_im': math.pi * n_idx + nrm(ks[6], gp, 0.01),
        's5_log_dt': jax.random.uniform(ks[7], (DEPTH, S5_GROUPS), f32, math.log(1e-3), math.log(1e-1)),
        's5_b_re': nrm(ks[8], (DEPTH, S5_GROUPS, S5_STATE, S5_GROUP), (2.0 * S5_GROUP) ** -0.5),
        's5_b_im': nrm(ks[9], (DEPTH, S5_GROUPS, S5_STATE, S5_GROUP), (2.0 * S5_GROUP) ** -0.5),
        's5_c_re': nrm(ks[10], (DEPTH, S5_GROUPS, S5_GROUP, S5_STATE), S5_STATE ** -0.5),
        's5_c_im': nrm(ks[11], (DEPTH, S5_GROUPS, S5_GROUP, S5_STATE), S5_STATE ** -0.5),
        's5_d': nrm(ks[12], (DEPTH, BRANCH_WIDTH), 1.0),
        's5_w_glu': nrm(ks[13], (DEPTH, BRANCH_WIDTH, BRANCH_WIDTH), BRANCH_WIDTH ** -0.5),
        's5_b_glu': nrm(ks[14], (DEPTH, BRANCH_WIDTH), 0.02),
        'w_gate': nrm(ks[15], (DEPTH, N_BRANCH, D_MODEL, D_MODEL), D_MODEL ** -0.5),
        'b_gate': nrm(ks[16], (DEPTH, N_BRANCH, D_MODEL), 0.02),
        'w_branch': nrm(ks[17], (DEPTH, N_BRANCH, BRANCH_WIDTH, D_MODEL), BRANCH_WIDTH ** -0.5),
        'w_out': nrm(ks[18], (DEPTH, D_MODEL, D_MODEL), D_MODEL ** -0.5 * DEEPNORM_BETA),
        'ln1_g': 1.0 + nrm(ks[19], (DEPTH, D_MODEL), 0.02),
        'ln1_b': nrm(ks[20], (DEPTH, D_MODEL), 0.02),
        'w_ffn_gate': nrm(ks[21], (DEPTH, D_MODEL, D_FF), D_MODEL ** -0.5),
        'w_ffn_up': nrm(ks[22], (DEPTH, D_MODEL, D_FF), D_MODEL ** -0.5),
        'w_ffn_down': nrm(ks[23], (DEPTH, D_FF, D_MODEL), D_FF ** -0.5 * DEEPNORM_BETA),
        'ln2_g': 1.0 + nrm(ks[24], (DEPTH, D_MODEL), 0.02),
        'ln2_b': nrm(ks[25], (DEPTH, D_MODEL), 0.02),
    }


def reference(x, w_in, gla_w_a, gla_b_a, att_rel_bias, s5_lambda_re, s5_lambda_im, s5_log_dt,
              s5_b_re, s5_b_im, s5_c_re, s5_c_im, s5_d, s5_w_glu, s5_b_glu, w_gate, b_gate,
              w_branch, w_out, ln1_g, ln1_b, w_ffn_gate, w_ffn_up, w_ffn_down, ln2_g, ln2_b):
    bsz, seq = x.shape[:2]
    for l in range(DEPTH):
        h = x @ w_in[l]
        (rq, rk, rv, rg, aq, ak, av, gq, gk, gv, gr, ga, su) = _split_cols(h, IN_SIZES)
        o_ret = _retention(rq, rk, rv, rg)
        o_att = _chunk_attention(aq, ak, av, att_rel_bias[l])
        o_gla = _gla(gq, gk, gv, gr, ga, gla_w_a[l], gla_b_a[l])
        o_s5 = _s5(su, s5_lambda_re[l], s5_lambda_im[l], s5_log_dt[l], s5_b_re[l], s5_b_im[l],
                   s5_c_re[l], s5_c_im[l], s5_d[l], s5_w_glu[l], s5_b_glu[l])
        branches = (o_ret, o_att, o_gla, o_s5)
        mixed = jax.nn.sigmoid(x @ w_gate[l, 0] + b_gate[l, 0]) * (branches[0] @ w_branch[l, 0])
        for bi in range(1, N_BRANCH):
            gate = jax.nn.sigmoid(x @ w_gate[l, bi] + b_gate[l, bi])
            mixed = mixed + gate * (branches[bi] @ w_branch[l, bi])
        x = _layer_norm(DEEPNORM_ALPHA * x + mixed @ w_out[l], ln1_g[l], ln1_b[l])
        ffn = (jax.nn.silu(x @ w_ffn_gate[l]) * (x @ w_ffn_up[l])) @ w_ffn_down[l]
        x = _layer_norm(DEEPNORM_ALPHA * x + ffn, ln2_g[l], ln2_b[l])
    return x
```

```python
import contextlib
import math
import numpy as np
import ml_dtypes
import concourse.bass as bass
import concourse.mybir as mybir
from concourse.bass_utils import run_bass_kernel_spmd

F32 = mybir.dt.float32
BF16 = mybir.dt.bfloat16
AF = mybir.ActivationFunctionType
ALU = mybir.AluOpType
AX = mybir.AxisListType

D = 1024
SEQ = 2048
DEPTH = 4
DFF = 2816
NFF = 22
ALPHA = (2.0 * DEPTH) ** 0.25
EPS = 1e-5
INW = 2576
TWO_PI = 6.283185307179586
MAGIC = 12582912.0


def _isz(dt):
    s = str(dt)
    if '32' in s:
        return 4
    if '16' in s:
        return 2
    if '8' in s:
        return 1
    return 4


def ap_range(ap):
    pairs = ap.ap
    off = int(ap.offset)
    sp = str(ap.space)
    if sp in ('SB', 'PSUM'):
        pstep = pairs[0][0]
        free = pairs[1:]
        base = off % pstep if pstep > 0 else off
    else:
        free = pairs
        base = off
    lo = base
    hi = base
    for st, n in free:
        if st >= 0:
            hi += st * (n - 1)
        else:
            lo += st * (n - 1)
    isz = _isz(ap.dtype)
    lo_b, hi_b = lo * isz, (hi + 1) * isz
    if sp == 'PSUM':
        lo_b = (lo_b // 2048) * 2048
        hi_b = ((hi_b + 2047) // 2048) * 2048
    return ap.tensor.name, lo_b, hi_b


class Sched:
    def __init__(self, nc):
        self.nc = nc
        self.eng = {'pe': nc.tensor, 'dve': nc.vector, 'act': nc.scalar,
                    'pool': nc.gpsimd, 'sp': nc.sync}
        self.sems = {}
        self.cnt = {}
        self.vcs = {}
        self.mult = {}
        self.know = {e: {} for e in self.eng}
        self.segs = {}
        self._ctx = []
        self.nwaits = 0
        self.nops = 0
        self.dead = False
        self.streams = {}
        self.dma_procs = set()
        for e in ('pe', 'dve', 'act', 'pool'):
            self.add_proc(e, 1)

    NSLOT = 16

    def add_stream(self, name):
        self.streams[name] = 0
        for i in range(self.NSLOT):
            self.add_proc('%s#%d' % (name, i), 16)
            self.dma_procs.add('%s#%d' % (name, i))

    def add_proc(self, name, mult=16):
        cm = self.nc.semaphore('s_' + name)
        s = cm.__enter__()
        self._ctx.append(cm)
        self.sems[name] = s
        self.cnt[name] = 0
        self.vcs[name] = [None]
        self.mult[name] = mult

    def close(self):
        for cm in reversed(self._ctx):
            cm.__exit__(None, None, None)

    def _collect(self, ap, is_write, deps):
        name, lo, hi = ap_range(ap)
        L = self.segs.get(name)
        if L is None:
            L = []
            self.segs[name] = L
        if is_write:
            new = []
            for s in L:
                if s[1] <= lo or s[0] >= hi:
                    new.append(s)
                    continue
                w = s[2]
                if w is not None and deps.get(w[0], 0) < w[1]:
                    deps[w[0]] = w[1]
                for p, i in s[3].items():
                    if deps.get(p, 0) < i:
                        deps[p] = i
                if s[0] < lo:
                    new.append([s[0], lo, s[2], dict(s[3])])
                if s[1] > hi:
                    new.append([hi, s[1], s[2], dict(s[3])])
            self.segs[name] = new
        else:
            for s in L:
                if s[1] <= lo or s[0] >= hi:
                    continue
                w = s[2]
                if w is not None and deps.get(w[0], 0) < w[1]:
                    deps[w[0]] = w[1]
        return (name, lo, hi)

    def op(self, e, fn, reads=(), writes=(), proc=None):
        if self.dead:
            return None
        proc = proc or e
        deps = {}
        if proc in self.streams:
            i = self.streams[proc]
            self.streams[proc] = i + 1
            proc = '%s#%d' % (proc, i % self.NSLOT)
            if self.cnt[proc] > 0:
                deps[proc] = self.cnt[proc]
        rr = [self._collect(ap, False, deps) for ap in reads if str(ap.space) != 'PSUM']
        ww = [self._collect(ap, True, deps) for ap in list(writes) + [a for a in reads if str(a.space) == 'PSUM']]
        k = self.know[e]
        eng = self.eng[e]
        for p, i in deps.items():
            if p == 'pe' and e == 'pe':
                continue
            if k.get(p, 0) >= i:
                continue
            eng.wait_ge(self.sems[p], i * self.mult[p])
            self.nwaits += 1
            vc = self.vcs[p][i]
            for q, j in vc.items():
                if k.get(q, 0) < j:
                    k[q] = j
        ins = fn()
        idx = self.cnt[proc] + 1
        self.cnt[proc] = idx
        ins.then_inc(self.sems[proc], self.mult[proc])
        vc = dict(k)
        vc[proc] = idx
        self.vcs[proc].append(vc)
        self.nops += 1
        me = (proc, idx)
        for (name, lo, hi) in ww:
            self.segs[name].append([lo, hi, me, {}])
        for (name, lo, hi) in rr:
            for s in self.segs[name]:
                if s[1] <= lo or s[0] >= hi:
                    continue
                if s[3].get(proc, 0) < idx:
                    s[3][proc] = idx
        return ins

    def wait_all(self, e, procs):
        eng = self.eng[e]
        for st in procs:
            for p in ['%s#%d' % (st, i) for i in range(self.NSLOT)]:
                if self.cnt[p] > 0:
                    eng.wait_ge(self.sems[p], self.cnt[p] * self.mult[p])

    def mm(self, out, lhsT, rhs, start=True, stop=True, **kw):
        nc = self.nc
        if 'tile_position' not in kw:
            bp = lhsT.base_partition()
            if bp not in (0, 32, 64):
                kw['tile_position'] = (bp, 0)
        return self.op('pe', lambda: nc.tensor.matmul(out, lhsT, rhs, start=start, stop=stop, **kw),
                       reads=[lhsT, rhs], writes=[out])

    def tr(self, out, in_, ident):
        nc = self.nc
        return self.op('pe', lambda: nc.tensor.transpose(out, in_, ident), reads=[in_, ident], writes=[out])

    def act(self, out, in_, func, bias=None, scale=None):
        nc = self.nc
        kw = {}
        rd = [in_]
        if bias is not None:
            kw['bias'] = bias
            if not isinstance(bias, (int, float)):
                rd.append(bias)
        if scale is not None:
            kw['scale'] = scale
            if not isinstance(scale, (int, float)):
                rd.append(scale)
        return self.op('act', lambda: nc.scalar.activation(out=out, in_=in_, func=func, **kw),
                       reads=rd, writes=[out])

    def tt(self, e, out, a, b, op):
        en = self.eng[e]
        return self.op(e, lambda: en.tensor_tensor(out=out, in0=a, in1=b, op=op), reads=[a, b], writes=[out])

    def ts(self, e, out, a, s1, s2, op0, op1=None):
        en = self.eng[e]
        rd = [a]
        for s in (s1, s2):
            if s is not None and not isinstance(s, (int, float)):
                rd.append(s)
        if op1 is None:
            return self.op(e, lambda: en.tensor_scalar(out=out, in0=a, scalar1=s1, scalar2=None, op0=op0),
                           reads=rd, writes=[out])
        return self.op(e, lambda: en.tensor_scalar(out=out, in0=a, scalar1=s1, scalar2=s2, op0=op0, op1=op1),
                       reads=rd, writes=[out])

    def stt(self, e, out, in0, scalar, in1, op0, op1):
        en = self.eng[e]
        rd = [in0, in1]
        if not isinstance(scalar, (int, float)):
            rd.append(scalar)
        return self.op(e, lambda: en.scalar_tensor_tensor(out=out, in0=in0, scalar=scalar, in1=in1, op0=op0, op1=op1),
                       reads=rd, writes=[out])

    def cp(self, e, out, in_):
        if e == 'act':
            nc = self.nc
            return self.op('act', lambda: nc.scalar.copy(out=out, in_=in_), reads=[in_], writes=[out])
        en = self.eng[e]
        return self.op(e, lambda: en.tensor_copy(out=out, in_=in_), reads=[in_], writes=[out])

    def memset(self, e, ap, val):
        en = self.eng[e]
        return self.op(e, lambda: en.memset(ap, val), reads=[], writes=[ap])

    def red(self, e, out, in_, op, axis=AX.X):
        en = self.eng[e]
        return self.op(e, lambda: en.tensor_reduce(out=out, in_=in_, axis=axis, op=op), reads=[in_], writes=[out])

    def scan(self, out, d0, d1, init, op0, op1):
        nc = self.nc
        rd = [d0, d1]
        if not isinstance(init, (int, float)):
            rd.append(init)
        return self.op('dve', lambda: nc.vector.tensor_tensor_scan(out=out, data0=d0, data1=d1, initial=init, op0=op0, op1=op1),
                       reads=rd, writes=[out])

    def recip(self, out, in_):
        nc = self.nc
        return self.op('dve', lambda: nc.vector.reciprocal(out=out, in_=in_), reads=[in_], writes=[out])

    def dma(self, stream, out, in_, e='sp', slow=False):
        en = self.eng[e]
        if slow:
            return self.op(e, lambda: en.dma_start(out=out, in_=in_, allow_slow_non_contiguous=True),
                           reads=[in_], writes=[out], proc=stream)
        return self.op(e, lambda: en.dma_start(out=out, in_=in_), reads=[in_], writes=[out], proc=stream)


class Arena:
    def __init__(self, t):
        self.t = t
        self.n = t.shape[1]
        self.pos = 0

    def reset(self):
        self.pos = 0

    def f32(self, n):
        a = self.t[:, self.pos:self.pos + n]
        self.pos += n
        assert self.pos <= self.n, ("arena overflow", self.pos, self.n)
        return a

    def b16(self, n):
        m = (n + 1) // 2
        a = self.t[:, self.pos:self.pos + m].bitcast(BF16)
        self.pos += m
        assert self.pos <= self.n, ("arena overflow", self.pos, self.n)
        return a


O_RQ, O_RK, O_RV, O_RG = 0, 128, 256, 512
O_AQ, O_AK, O_AV = 768, 1024, 1280
O_GQ, O_GK, O_GV, O_GR, O_GA, O_SU = 1536, 1664, 1792, 2048, 2304, 2320
PC_BG = 0
PC_NBA = 128
PC_SD = 132
PC_BGLU = 140
PC_RHO = 148
PC_RDEC = 180
PC_BD4 = 181
PRM_W = 192


def host_consts():
    c = {}
    c['c_ident'] = np.eye(128, dtype=np.float32)
    c['c_anti'] = np.eye(128, dtype=np.float32)[::-1].copy()
    pos = np.arange(SEQ, dtype=np.float64)
    inv = 1.0 / (10000.0 ** (np.arange(16, dtype=np.float64) / 16.0))
    hd = np.arange(128)
    h_of = hd // 32
    d_of = hd % 32
    j_of = d_of % 16
    sign = np.where(d_of < 16, -1.0, 1.0)
    ang = pos[None, :] * inv[j_of][:, None]
    c['c_rcos'] = np.cos(ang).astype(np.float32)
    c['c_rsin'] = (np.sin(ang) * sign[:, None]).astype(np.float32)
    gam = 1.0 - 2.0 ** (-5.0 - np.arange(4, dtype=np.float64))
    lg = np.log(gam)
    p = np.arange(128, dtype=np.float64)
    c['c_rxi'] = np.exp(lg[h_of][:, None] * (p[None, :] + 1.0)).astype(np.float32)
    posm = (np.arange(16)[None, :, None] * 128 + np.arange(128)[:, None, None]).astype(np.float64)
    angk = posm * inv[j_of][None, None, :]
    zeta = np.exp(lg[h_of][None, None, :] * (127.0 - np.arange(128, dtype=np.float64))[:, None, None])
    sc = 32.0 ** -0.5
    c['c_rck'] = (np.cos(angk) * zeta * sc).astype(np.float32)
    c['c_rsk'] = (np.sin(angk) * sign[None, None, :] * zeta * sc).astype(np.float32)
    k_ = np.arange(128)[:, None]
    q_ = np.arange(128)[None, :]
    ck, cq = k_ // 64, q_ // 64
    m = np.zeros((128, 4, 128), np.float64)
    for h in range(4):
        same = np.exp(lg[h] * np.abs(q_ - k_))
        cross = np.exp(lg[h] * (q_ - k_).clip(min=0))
        m[:, h, :] = np.where(ck == cq, same, np.where((cq == 1) & (ck == 0), cross, 0.0)) * sc
    c['c_rmask'] = m.reshape(128, 512).astype(np.float32)
    misc = np.zeros((128, 8), np.float32)
    misc[:, 0] = np.exp(lg[h_of] * 128.0)
    for h in range(4):
        misc[:, 1 + h] = (h_of == h)
    c['c_misc'] = misc
    gm = np.ones((128, SEQ), np.float32)
    gm[:, ::64] = 0.0
    c['c_gmask'] = gm.astype(ml_dtypes.bfloat16)
    c['c_iota'] = np.broadcast_to(np.arange(SEQ, dtype=np.float32)[None, :], (128, SEQ)).copy()
    s_ = np.arange(128)[:, None, None]
    kk = np.arange(8)[None, :, None]
    cc = np.arange(128)[None, None, :]
    c['c_s5mask'] = ((cc // 16) == ((2 * kk + s_ // 64) % 8)).astype(np.float32).reshape(128, 1024)
    return c


CONST_SPECS = [
    ('c_ident', [128, 128], F32), ('c_anti', [128, 128], F32),
    ('c_rcos', [128, SEQ], F32), ('c_rsin', [128, SEQ], F32), ('c_rxi', [128, 128], F32),
    ('c_rck', [128, 16, 128], F32), ('c_rsk', [128, 16, 128], F32), ('c_rmask', [128, 512], F32),
    ('c_misc', [128, 8], F32), ('c_gmask', [128, SEQ], BF16), ('c_iota', [128, SEQ], F32),
    ('c_s5mask', [128, 1024], F32),
]

IN_SPECS = [
    ('w_in', [DEPTH, D, INW]), ('gla_w_a', [DEPTH, 16, 128]), ('gla_b_a', [DEPTH, 128]),
    ('att_rel_bias', [DEPTH, 4, 257]), ('s5_lambda_re', [DEPTH, 16, 64]), ('s5_lambda_im', [DEPTH, 16, 64]),
    ('s5_log_dt', [DEPTH, 16]), ('s5_b_re', [DEPTH, 16, 64, 16]), ('s5_b_im', [DEPTH, 16, 64, 16]),
    ('s5_c_re', [DEPTH, 16, 16, 64]), ('s5_c_im', [DEPTH, 16, 16, 64]), ('s5_d', [DEPTH, 256]),
    ('s5_w_glu', [DEPTH, 256, 256]), ('s5_b_glu', [DEPTH, 256]), ('w_gate', [DEPTH, 4, D, D]),
    ('b_gate', [DEPTH, 4, D]), ('w_branch', [DEPTH, 4, 256, D]), ('w_out', [DEPTH, D, D]),
    ('ln1_g', [DEPTH, D]), ('ln1_b', [DEPTH, D]), ('w_ffn_gate', [DEPTH, D, DFF]),
    ('w_ffn_up', [DEPTH, D, DFF]), ('w_ffn_down', [DEPTH, DFF, D]), ('ln2_g', [DEPTH, D]), ('ln2_b', [DEPTH, D]),
]


class _Stop(Exception):
    pass


def build(NSEQ=4, NL=DEPTH, dbg=None, stop=None):
    nc = bass.Bass("TRN2", target_bir_lowering=False)
    S = Sched(nc)
    for st_ in ('w', 'io', 'pre'):
        S.add_stream(st_)
    I = {}
    I['x'] = nc.dram_tensor("x", [NSEQ * SEQ, D], F32, kind="ExternalInput").ap()
    for name, shp in IN_SPECS:
        I[name] = nc.dram_tensor(name, shp, F32, kind="ExternalInput").ap()
    C = {}
    for name, shp, dt in CONST_SPECS:
        C[name] = nc.dram_tensor(name, shp, dt, kind="ExternalInput").ap()
    Y = nc.dram_tensor("y", [NSEQ * SEQ, D], F32, kind="ExternalOutput").ap()
    DBG = None
    if dbg:
        DBG = nc.dram_tensor("dbg", [128, 8, SEQ], F32, kind="ExternalOutput").ap()

    def scr(name, shp, dt=BF16):
        return nc.dram_tensor(name, shp, dt, kind="Internal").ap()

    wc_in = scr("wc_in", [NL, 13, 128, 8 * 128])
    wn_in = scr("wn_in", [NL, 128, 8, 1536])
    wc_gate = scr("wc_gate", [NL, 4, 8, 128, 8 * 128])
    wc_br = scr("wc_br", [NL, 4, 8, 128, 2 * 128])
    wc_glu = scr("wc_glu", [NL, 2, 128, 2 * 128])
    wn_out = scr("wn_out", [NL, 8, 128, 1024])
    wc_fg = scr("wc_fg", [NL, NFF, 128, 8 * 128])
    wc_fu = scr("wc_fu", [NL, NFF, 128, 8 * 128])
    wn_fd = scr("wn_fd", [NL, NFF, 128, 1024])
    s5_tab = scr("s5_tab", [NL, 8, 2, 128, SEQ], F32)
    s5_lhs = scr("s5_lhs", [NL, 128, 8 * 4 * 128])
    att_E = scr("att_E", [NL, 4, 768], F32)
    att_BT = scr("att_BT", [NL, 128, 5 * 4 * 128])

    es = contextlib.ExitStack()
    with es:
        def sb(name, shape, dt):
            return es.enter_context(nc.sbuf_tensor(name, shape, dt))

        def ps(name, shape, dt):
            return es.enter_context(nc.psum_tensor(name, shape, dt))

        X32 = sb("X32", [128, 16, D], F32)
        XT = sb("XT", [128, 8, SEQ], BF16)
        BR = sb("BR", [128, 8, SEQ], BF16)
        SCR = sb("SCR", [128, 15104], F32)
        WR = sb("WR", [128, 8192], BF16)
        CB = sb("CB", [128, 256], BF16)
        PRM = sb("PRM", [128, PRM_W], F32)
        GW = sb("GW", [16, 512], BF16)
        PP = ps("PP", [128, 8, 512], F32)
        P = [PP[:, i, :] for i in range(8)]
        A = Arena(SCR)
        ident = CB[:, 0:128]
        anti = CB[:, 128:256]
        PB = P[7].bitcast(BF16)

        def slot(i, n=1):
            return WR[:, i * 1024:(i + n) * 1024]

        def chk(name):
            if stop == name and not S.dead:
                DF = DBG.rearrange("p a n -> p (a n)")
                if name == 'pro2':
                    S.dma('io', DF[:, 0:4096], SCR[:, 0:4096])
                    S.dma('io', DF[:, 4096:8192], BR[:, 0:4, :].rearrange("p a b -> p (a b)").bitcast(F32))
                    S.dma('io', DF[:, 8192:10240], s5_tab[0, 7, 0])
                    S.dma('io', DF[:, 10240:12288], s5_tab[0, 7, 1])
                    S.dead = True
                    return
                S.dma('io', DF[:, 0:15104], SCR[:, :])
                S.dma('io', DF[:, 15104:15104 + PRM_W], PRM[:, :])
                S.dead = True

        def eng_rr(i, choices=('act', 'pool', 'dve')):
            return choices[i % len(choices)]

        def cast(e, out, in_):
            S.cp(e, out, in_)

        A.reset()
        t32 = A.f32(256)
        S.dma('pre', t32[:, 0:128], C['c_ident'])
        S.dma('pre', t32[:, 128:256], C['c_anti'])
        S.cp('dve', CB[:, :], t32)
        identf = A.f32(128)
        S.dma('pre', identf, C['c_ident'])
        S.dma('pre', PRM[:, PC_RDEC:PC_RDEC + 8], C['c_misc'])

        stg = A.f32(128)

        def transp_rows(nrows, dst_cols, post=None):
            S.op('pe', lambda: nc.tensor.transpose(P[0][:, 0:nrows], stg[0:nrows, :], identf[0:nrows, 0:nrows]),
                 reads=[stg[0:nrows, :], identf], writes=[P[0][:, 0:nrows]])
            if post is None:
                S.cp('dve', dst_cols, P[0][:, 0:nrows])
            else:
                S.ts('dve', dst_cols, P[0][:, 0:nrows], post, None, ALU.mult)

        nr = NL * 32
        S.dma('pre', stg[0:nr, :], I['b_gate'][0:NL].rearrange("l b (c p) -> (l b c) p", p=128))
        transp_rows(nr, PRM[:, PC_BG:PC_BG + nr])
        S.dma('pre', stg[0:NL, :], I['gla_b_a'][0:NL])
        transp_rows(NL, PRM[:, PC_NBA:PC_NBA + NL], post=-1.0)
        S.dma('pre', stg[0:2 * NL, :], I['s5_d'][0:NL].rearrange("l (h p) -> (l h) p", p=128))
        transp_rows(2 * NL, PRM[:, PC_SD:PC_SD + 2 * NL])
        S.dma('pre', stg[0:2 * NL, :], I['s5_b_glu'][0:NL].rearrange("l (h p) -> (l h) p", p=128))
        transp_rows(2 * NL, PRM[:, PC_BGLU:PC_BGLU + 2 * NL])
        NLK = NL * 8
        s5p = A.f32(NLK * 16)

        def s5c(i):
            return s5p[:, i * NLK:(i + 1) * NLK]
        LR, LI, DT, AA, PH, RHO_, ABR, ABI, DEN, NRr, CRE, CIM, TMP1, TMP2, NCIM, TMP3 = [s5c(i) for i in range(16)]
        S.dma('pre', stg[0:NLK, :], I['s5_lambda_re'][0:NL].rearrange("l (k g) p -> (l k) (g p)", g=2))
        S.op('pe', lambda: nc.tensor.transpose(P[0][:, 0:NLK], stg[0:NLK, :], identf[0:NLK, 0:NLK]),
             reads=[stg[0:NLK, :], identf], writes=[P[0][:, 0:NLK]])
        S.cp('dve', LR, P[0][:, 0:NLK])
        S.dma('pre', stg[0:NLK, :], I['s5_lambda_im'][0:NL].rearrange("l (k g) p -> (l k) (g p)", g=2))
        S.op('pe', lambda: nc.tensor.transpose(P[0][:, 0:NLK], stg[0:NLK, :], identf[0:NLK, 0:NLK]),
             reads=[stg[0:NLK, :], identf], writes=[P[0][:, 0:NLK]])
        S.cp('dve', LI, P[0][:, 0:NLK])
        ldt2 = A.f32(2)
        S.dma('pre', ldt2[0:NLK, :], I['s5_log_dt'][0:NL].rearrange("l (k g) -> (l k) g", g=2))
        S.cp('dve', stg[0:NLK, :].rearrange("r (g p) -> r g p", g=2), ldt2[0:NLK, :].unsqueeze(2).to_broadcast([NLK, 2, 64]))
        S.op('pe', lambda: nc.tensor.transpose(P[0][:, 0:NLK], stg[0:NLK, :], identf[0:NLK, 0:NLK]),
             reads=[stg[0:NLK, :], identf], writes=[P[0][:, 0:NLK]])
        XL = TMP3
        S.cp('dve', XL, P[0][:, 0:NLK])
        KF = TMP1
        S.ts('dve', KF, XL, 1.4426950408889634, None, ALU.mult)
        S.ts('dve', KF, KF, MAGIC, None, ALU.add)
        S.ts('dve', KF, KF, -MAGIC, None, ALU.add)
        RR = TMP2
        S.stt('dve', RR, KF, -0.693145751953125, XL, ALU.mult, ALU.add)
        S.stt('dve', RR, KF, -1.4286068203094172e-06, RR, ALU.mult, ALU.add)
        EE = DEN
        S.memset('dve', EE, 1.0)
        for j in range(10, 0, -1):
            S.stt('dve', EE, EE, 1.0 / j, RR, ALU.mult, ALU.mult)
            S.ts('dve', EE, EE, 1.0, None, ALU.add)
        P2 = ABR
        S.memset('dve', P2, 0.25)
        for j in range(3, 14):
            S.ts('dve', NRr, KF, float(-j), -0.5, ALU.is_le, ALU.mult)
            S.stt('dve', P2, NRr, 1.0, P2, ALU.add, ALU.mult)
        S.tt('dve', DT, P2, EE, ALU.mult)
        S.tt('dve', AA, LR, DT, ALU.mult)
        S.tt('dve', PH, LI, DT, ALU.mult)
        EM1 = TMP3
        S.memset('dve', EE, 1.0)
        for j in range(8, 1, -1):
            S.stt('dve', EE, EE, 1.0 / j, AA, ALU.mult, ALU.mult)
            S.ts('dve', EE, EE, 1.0, None, ALU.add)
        S.tt('dve', EM1, EE, AA, ALU.mult)
        S.ts('dve', RHO_, EM1, 1.0, None, ALU.add)
        S.cp('dve', PRM[:, PC_RHO:PC_RHO + NLK], RHO_)

        def sin_of(out, ang, mul, tmpa, tmpb):
            S.ts('dve', tmpa, ang, mul / TWO_PI, None, ALU.mult)
            S.ts('dve', tmpb, tmpa, MAGIC, None, ALU.add)
            S.ts('dve', tmpb, tmpb, -MAGIC, None, ALU.add)
            S.tt('dve', tmpa, tmpa, tmpb, ALU.subtract)
            S.act(out, tmpa, AF.Sin, scale=6.28318)
        SH = ABR
        sin_of(SH, PH, 0.5, TMP1, TMP2)
        sin_of(ABI, PH, 1.0, TMP1, TMP2)
        S.tt('dve', ABI, ABI, RHO_, ALU.mult)
        S.tt('dve', SH, SH, SH, ALU.mult)
        S.tt('dve', SH, SH, RHO_, ALU.mult)
        S.stt('dve', NRr, SH, -2.0, EM1, ALU.mult, ALU.add)
        S.tt('dve', DEN, LR, LR, ALU.mult)
        S.tt('dve', TMP1, LI, LI, ALU.mult)
        S.tt('dve', DEN, DEN, TMP1, ALU.add)
        S.recip(DEN, DEN)
        S.tt('dve', TMP1, NRr, LR, ALU.mult)
        S.tt('dve', TMP2, ABI, LI, ALU.mult)
        S.tt('dve', TMP1, TMP1, TMP2, ALU.add)
        S.tt('dve', CRE, TMP1, DEN, ALU.mult)
        S.tt('dve', TMP1, ABI, LR, ALU.mult)
        S.tt('dve', TMP2, NRr, LI, ALU.mult)
        S.tt('dve', TMP1, TMP1, TMP2, ALU.subtract)
        S.tt('dve', CIM, TMP1, DEN, ALU.mult)
        S.ts('dve', NCIM, CIM, -1.0, None, ALU.mult)

        chk('pro1')
        gwf = A.f32(NL * 128)
        S.dma('pre', gwf[0:16, :].rearrange("r (l c) -> r l c", l=NL), I['gla_w_a'][0:NL].rearrange("l r c -> r l c"))
        S.cp('dve', GW[0:16, 0:NL * 128], gwf[0:16, :])

        s5m = A.f32(1024)
        S.dma('pre', s5m, C['c_s5mask'])
        iota = A.f32(SEQ)
        S.dma('pre', iota, C['c_iota'])
        brt = A.f32(32)
        bbx = A.b16(256)
        lhs_stage = A.b16(8 * 4 * 128)
        tS = BR[:, 0:4, :].rearrange("p a b -> p (a b)").bitcast(F32)
        tA = BR[:, 4:6, :].rearrange("p a b -> p (a b)").bitcast(F32)
        tB = BR[:, 6:8, :].rearrange("p a b -> p (a b)").bitcast(F32)
        cld = A.f32(256)
        cldb = A.b16(256)
        ct2 = A.b16(512)
        for l in range(NL):
            for ri, nm in enumerate(('s5_c_re', 's5_c_im')):
                for ft in range(2):
                    src = I[nm][l, 8 * ft:8 * ft + 8].rearrange("g i p -> (g i) p")
                    S.dma('pre', cld[:, 0:64], src)
                    S.dma('pre', cld[:, 64:128], src)
                    S.cp('dve', cldb[:, 0:128], cld[:, 0:128])
                    S.tr(PB[:, 0:128], cldb[:, 0:128], ident)
                    S.cp('act', ct2[:, (ri * 2 + ft) * 128:(ri * 2 + ft + 1) * 128], PB[:, 0:128])
            for k in range(8):
                lk = l * 8 + k
                S.dma('pre', brt[:, 0:16], I['s5_b_re'][l, 2 * k:2 * k + 2].rearrange("g p i -> (g p) i"))
                S.dma('pre', brt[:, 16:32], I['s5_b_im'][l, 2 * k:2 * k + 2].rearrange("g p i -> (g p) i"))
                bbr = tA[:, 0:16]
                bbi = tA[:, 16:32]
                tq = tA[:, 32:48]
                S.ts('dve', tq, brt[:, 0:16], CRE[:, lk:lk + 1], None, ALU.mult)
                S.stt('dve', bbr, brt[:, 16:32], NCIM[:, lk:lk + 1], tq, ALU.mult, ALU.add)
                S.ts('dve', tq, brt[:, 16:32], CRE[:, lk:lk + 1], None, ALU.mult)
                S.stt('dve', bbi, brt[:, 0:16], CIM[:, lk:lk + 1], tq, ALU.mult, ALU.add)
                mk = s5m[:, k * 128:(k + 1) * 128]
                for j, bb in enumerate((bbr, bbi)):
                    S.tt('dve', bbx[:, 0:128].rearrange("p (r i) -> p r i", r=8), mk.rearrange("p (r i) -> p r i", r=8),
                         bb.unsqueeze(1).to_broadcast([128, 8, 16]), ALU.mult)
                    S.tr(PB[:, 0:128], bbx[:, 0:128], ident)
                    S.cp('act', lhs_stage[:, (k * 4 + j) * 128:(k * 4 + j + 1) * 128], PB[:, 0:128])
                ft = k // 4
                S.tt('dve', lhs_stage[:, (k * 4 + 2) * 128:(k * 4 + 3) * 128], ct2[:, ft * 128:(ft + 1) * 128], mk, ALU.mult)
                S.stt('dve', lhs_stage[:, (k * 4 + 3) * 128:(k * 4 + 4) * 128], ct2[:, (2 + ft) * 128:(3 + ft) * 128],
                      -1.0, mk, ALU.mult, ALU.mult)
                S.ts('dve', tS[:, 0:SEQ], iota, PH[:, lk:lk + 1], None, ALU.mult)
                for (o, shift) in ((tS[:, SEQ:2 * SEQ], 0.0), (tS[:, 0:SEQ], 0.25)):
                    pass
                ang = tS[:, 0:SEQ]
                sin_o = tS[:, SEQ:2 * SEQ]
                S.ts('dve', tA, ang, 1.0 / TWO_PI, None, ALU.mult)
                S.ts('dve', tB, tA, MAGIC, None, ALU.add)
                S.ts('dve', tB, tB, -MAGIC, None, ALU.add)
                S.tt('pool', tB, tA, tB, ALU.subtract)
                S.act(sin_o, tB, AF.Sin, scale=6.28318)
                S.ts('dve', tA, tA, 0.25, None, ALU.add)
                S.ts('dve', tB, tA, MAGIC, None, ALU.add)
                S.ts('dve', tB, tB, -MAGIC, None, ALU.add)
                S.tt('pool', tB, tA, tB, ALU.subtract)
                S.act(ang, tB, AF.Sin, scale=6.28318)
                S.dma('pre', s5_tab[l, k, 0], ang)
                S.dma('pre', s5_tab[l, k, 1], sin_o)
            S.dma('pre', s5_lhs[l], lhs_stage)
            esb = tB[0:4, 0:768]
            S.dma('pre', esb[:, 0:256], I['att_rel_bias'][l, :, 1:257])
            S.cp('dve', esb[:, 256:768], esb[:, 255:256].to_broadcast([4, 512]))
            S.dma('pre', att_E[l], esb)
        chk('pro2')
        btf = tS[:, 0:2560]
        btb = A.b16(2560)
        for l in range(NL):
            for t5 in range(5):
                src = bass.AP(att_E.tensor, l * 4 * 768 + 128 * t5, [[1, 128], [768, 4], [1, 128]])
                S.dma('pre', btf[:, t5 * 512:(t5 + 1) * 512].rearrange("p (h q) -> p h q", h=4), src)
            S.cp('dve', btb, btf)
            v = btb.rearrange("p (t h q) -> p t h q", t=5, h=4)
            S.memset('pool', v[64:128, 4, :, 64:128], -30000.0)
            S.memset('pool', v[0:64, 0, :, 0:64], -30000.0)
            S.dma('pre', att_BT[l], btb)

        chk('pro3')
        cvt_i = [0]

        def stage():
            i = cvt_i[0] % 4
            cvt_i[0] += 1
            f = X32[:, 4 * i:4 * i + 4, :].rearrange("p a b -> p (a b)")
            b = XT[:, 2 * i:2 * i + 2, :].rearrange("p a b -> p (a b)")
            return f, b, cvt_i[0]

        def conv_colT(src, K, ncols, dst_tiles, perm=False):
            f, b, i = stage()
            nt = ncols // 128 if ncols >= 128 else 1
            cw = min(ncols, 128)
            fv = f[:, 0:K * ncols].rearrange("p (k c) -> p k c", k=K)
            S.dma('pre', fv, src.rearrange("(k p) c -> p k c", p=128))
            e = eng_rr(i)
            for t in range(nt):
                bv = b[:, t * K * 128:(t + 1) * K * 128].rearrange("p (k c) -> p k c", k=K)
                if not perm:
                    cast(e, bv[:, :, 0:cw], fv[:, :, t * 128:t * 128 + cw])
                else:
                    fo = fv[:, :, t * 128:(t + 1) * 128].rearrange("p k (h w j) -> p k h w j", h=4, w=2)
                    bo = bv.rearrange("p k (h w j) -> p k h w j", h=4, w=2)
                    for hh in range(4):
                        cast(e, bo[:, :, hh, 0, :], fo[:, :, hh, 1, :])
                        cast(e, bo[:, :, hh, 1, :], fo[:, :, hh, 0, :])
                S.dma('pre', dst_tiles[t], b[:, t * K * 128:(t + 1) * K * 128])

        def conv_nat(src, K, ncols, dst, perm=False):
            f, b, i = stage()
            fv = f[:, 0:K * ncols].rearrange("p (k c) -> p k c", k=K)
            bv = b[:, 0:K * ncols].rearrange("p (k c) -> p k c", k=K)
            S.dma('pre', fv, src.rearrange("(k p) c -> p k c", p=128))
            e = eng_rr(i)
            if not perm:
                cast(e, bv, fv)
            else:
                fo = fv.rearrange("p k (h w j) -> p k h w j", h=4, w=2)
                bo = bv.rearrange("p k (h w j) -> p k h w j", h=4, w=2)
                for hh in range(4):
                    cast(e, bo[:, :, hh, 0, :], fo[:, :, hh, 1, :])
                    cast(e, bo[:, :, hh, 1, :], fo[:, :, hh, 0, :])
            S.dma('pre', dst, bv)

        for l in range(NL):
            W = I['w_in'][l]
            ci = wc_in[l]
            conv_colT(W[:, O_RQ:O_RQ + 128], 8, 128, [ci[0]])
            conv_colT(W[:, O_RQ:O_RQ + 128], 8, 128, [ci[1]], perm=True)
            conv_colT(W[:, O_RK:O_RK + 128], 8, 128, [ci[2]])
            conv_colT(W[:, O_RK:O_RK + 128], 8, 128, [ci[3]], perm=True)
            conv_colT(W[:, O_AQ:O_AQ + 256], 8, 256, [ci[4], ci[5]])
            conv_colT(W[:, O_AK:O_AK + 256], 8, 256, [ci[6], ci[7]])
            conv_colT(W[:, O_GQ:O_GQ + 256], 8, 256, [ci[8], ci[9]])
            conv_colT(W[:, O_GA:O_GA + 16], 8, 16, [ci[10]])
            conv_colT(W[:, O_SU:O_SU + 256], 8, 256, [ci[11], ci[12]])
            wn = wn_in[l].rearrange("p (k c) -> p k c", k=8) if False else wn_in[l]
            conv_nat(W[:, O_RK:O_RK + 128], 8, 128, wn[:, :, 0:128])
            conv_nat(W[:, O_RK:O_RK + 128], 8, 128, wn[:, :, 128:256], perm=True)
            conv_nat(W[:, O_RV:O_RV + 512], 8, 512, wn[:, :, 256:768])
            conv_nat(W[:, O_AV:O_AV + 256], 8, 256, wn[:, :, 768:1024])
            conv_nat(W[:, O_GV:O_GV + 512], 8, 512, wn[:, :, 1024:1536])
            for b_ in range(4):
                for hf in range(2):
                    conv_colT(I['w_gate'][l, b_][:, hf * 512:(hf + 1) * 512], 8, 512,
                              [wc_gate[l, b_, hf * 4 + t] for t in range(4)])
                for hf in range(2):
                    conv_colT(I['w_branch'][l, b_][:, hf * 512:(hf + 1) * 512], 2, 512,
                              [wc_br[l, b_, hf * 4 + t] for t in range(4)])
            conv_colT(I['s5_w_glu'][l], 2, 256, [wc_glu[l, 0], wc_glu[l, 1]])
            for hf in range(2):
                for kh in range(2):
                    pass
            for hf in range(2):
                conv_nat(I['w_out'][l][:, hf * 512:(hf + 1) * 512], 8, 512,
                         wn_out[l].rearrange("k p c -> p k c")[:, :, hf * 512:(hf + 1) * 512])
            for (src_w, dst_w) in ((I['w_ffn_gate'][l], wc_fg[l]), (I['w_ffn_up'][l], wc_fu[l])):
                for g in range(6):
                    c0 = g * 512
                    ncol = min(512, DFF - c0)
                    conv_colT(src_w[:, c0:c0 + ncol], 8, ncol, [dst_w[g * 4 + t] for t in range(ncol // 128)])
            fd = I['w_ffn_down'][l]
            for (k0, kn) in ((0, 8), (8, 8), (16, 6)):
                for hf in range(2):
                    conv_nat(fd[k0 * 128:(k0 + kn) * 128, hf * 512:(hf + 1) * 512], kn, 512,
                             wn_fd[l, k0:k0 + kn].rearrange("k p c -> p k c")[:, :, hf * 512:(hf + 1) * 512])

        chk('conv')
        def load_w(dst, src, stream='w'):
            S.dma(stream, dst, src)

        def proj_T(pout, wtile, blk, M=128):
            wv = wtile.rearrange("p (k c) -> p k c", k=8)
            if not hasattr(pout, 'offset'):
                pout = pout[:, :]
            for k in range(8):
                S.mm(pout, wv[:, k, 0:M], XT[:, k, blk * 512:(blk + 1) * 512], start=(k == 0), stop=(k == 7))

        def proj_tok(pout, wnat, t, c0, ncol):
            for k in range(8):
                S.mm(pout, XT[:, k, t * 128:(t + 1) * 128], wnat[:, k, c0:c0 + ncol], start=(k == 0), stop=(k == 7))

        def make_xt(t, xb):
            S.cp('act', xb, X32[:, t, :])
            for c in range(8):
                S.tr(PB[:, c * 128:(c + 1) * 128], xb[:, c * 128:(c + 1) * 128], ident)
            S.cp('dve', XT[:, :, t * 128:(t + 1) * 128], PB.rearrange("p (c q) -> p c q", c=8))

        def layer_norm_multi(items, gam, bet):
            for (t, pbanks, st) in items:
                for hf in range(2):
                    xs = X32[:, t, hf * 512:(hf + 1) * 512]
                    S.stt('dve', xs, xs, ALPHA, pbanks[hf][:, :], ALU.mult, ALU.add)
            for (t, pbanks, st) in items:
                for hf in range(2):
                    xs = X32[:, t, hf * 512:(hf + 1) * 512]
                    S.op('dve', lambda xs=xs, hf=hf, st=st: nc.vector.bn_stats(out=st[:, 6 * hf:6 * hf + 6], in_=xs),
                         reads=[xs], writes=[st[:, 6 * hf:6 * hf + 6]])
                S.op('dve', lambda st=st: nc.vector.bn_aggr(out=st[:, 12:14], in_=st[:, 0:12].rearrange("p (a b) -> p a b", a=2)),
                     reads=[st[:, 0:12]], writes=[st[:, 12:14]])
            for (t, pbanks, st) in items:
                S.act(st[:, 14:15], st[:, 13:14], AF.Sqrt, bias=EPS)
            for (t, pbanks, st) in items:
                S.recip(st[:, 15:16], st[:, 14:15])
                S.stt('dve', st[:, 16:17], st[:, 12:13], -1.0, st[:, 15:16], ALU.mult, ALU.mult)
            for (t, pbanks, st) in items:
                xr = X32[:, t, :]
                S.act(xr, xr, AF.Identity, bias=st[:, 16:17], scale=st[:, 15:16])
            for (t, pbanks, st) in items:
                xr = X32[:, t, :]
                S.tt('dve', xr, xr, gam, ALU.mult)
            for (t, pbanks, st) in items:
                xr = X32[:, t, :]
                S.tt('pool', xr, xr, bet, ALU.add)

        def headnorm_gate(pO, sg, brow, t, wk):
            sq, st, on, ob = wk
            o3 = pO
            S.act(sq.rearrange("p (h e) -> p h e", h=4), o3, AF.Square)
            S.red('dve', st[:, 0:4], o3, ALU.add)
            S.red('dve', st[:, 4:8], sq.rearrange("p (h e) -> p h e", h=4), ALU.add)
            S.ts('dve', st[:, 8:12], st[:, 0:4], 1.0 / 64.0, None, ALU.mult)
            S.tt('dve', st[:, 12:16], st[:, 8:12], st[:, 8:12], ALU.mult)
            S.stt('dve', st[:, 16:20], st[:, 4:8], 1.0 / 64.0, st[:, 12:16], ALU.mult, ALU.subtract)
            S.act(st[:, 20:24], st[:, 16:20], AF.Sqrt, bias=EPS)
            S.recip(st[:, 24:28], st[:, 20:24])
            on3 = on.rearrange("p (h e) -> p h e", h=4)
            S.tt('dve', on3, o3, st[:, 8:12].unsqueeze(2).to_broadcast([128, 4, 64]), ALU.subtract)
            S.tt('pool', on3, on3, st[:, 24:28].unsqueeze(2).to_broadcast([128, 4, 64]), ALU.mult)
            S.tt('pool', ob, on, sg, ALU.mult)
            for hf in range(2):
                S.tr(PB[:, hf * 128:(hf + 1) * 128], ob[:, hf * 128:(hf + 1) * 128], ident)
            S.cp('act', BR[:, brow:brow + 2, t * 128:(t + 1) * 128], PB[:, 0:256].rearrange("p (c q) -> p c q", c=2))

        out_cnt = 0
        for s in range(NSEQ):
            S.dma('io', X32[:, :, :], I['x'][s * SEQ:(s + 1) * SEQ, :].rearrange("(t p) d -> p t d", p=128))
            A.reset()
            xb0 = A.b16(1024)
            for t in range(16):
                make_xt(t, xb0)
            for l in range(NL):
                wci = wc_in[l]
                wni = wn_in[l].rearrange("p k c -> p (k c)") if False else wn_in[l]
                chk('xt')
                A.reset()
                qT = A.b16(SEQ)
                kT = A.b16(SEQ)
                qxT = A.b16(SEQ)
                Rbf = A.b16(17 * 64)
                R32 = A.f32(17 * 64)
                rmask = A.f32(512)
                xi = A.f32(128)
                rc_ = A.f32(512)
                rs_ = A.f32(512)
                t1 = A.f32(512)
                t2 = A.f32(512)
                sm = A.f32(512)
                S.dma('io', rmask, C['c_rmask'])
                S.dma('io', xi, C['c_rxi'])
                for i_ in range(4):
                    load_w(slot(i_), wci[i_])
                S.memset('pool', R32[:, 0:64], 0.0)
                S.memset('pool', Rbf[:, 0:64], 0.0)
                for b in range(4):
                    bs = slice(b * 512, (b + 1) * 512)
                    S.dma('io', rc_, C['c_rcos'][:, bs])
                    S.dma('io', rs_, C['c_rsin'][:, bs])
                    for (w0, dstT, dox) in ((0, qT, True), (2, kT, False)):
                        proj_T(P[0], slot(w0), b)
                        proj_T(P[1], slot(w0 + 1), b)
                        S.tt('dve', t1, P[0][:, :], rc_, ALU.mult)
                        S.tt('dve', t2, P[1][:, :], rs_, ALU.mult)
                        S.tt('pool', sm, t1, t2, ALU.add)
                        S.cp('act', dstT[:, bs], sm)
                        if dox:
                            S.tt('pool', qxT[:, bs].rearrange("p (a q) -> p a q", a=4),
                                 sm.rearrange("p (a q) -> p a q", a=4),
                                 xi.unsqueeze(1).to_broadcast([128, 4, 128]), ALU.mult)
                wn1 = slot(0, 4).rearrange("p (k c) -> p k c", k=8)
                wn2 = slot(4, 2).rearrange("p (k c) -> p k c", k=8)
                load_w(wn1, wn_in[l][:, :, 0:512])
                load_w(wn2, wn_in[l][:, :, 512:768])
                ck_ = A.f32(128)
                sk_ = A.f32(128)
                a1 = A.f32(128)
                a2 = A.f32(128)
                kz = A.b16(128)
                vtok = A.b16(256)
                sg = A.f32(256)
                sTm = A.b16(512)
                tmpu = A.f32(256)
                upd = A.f32(64)
                hn_wk = (A.f32(256), A.f32(32), A.f32(256), A.b16(256))
                for t in range(16):
                    tk = slice(t * 128, (t + 1) * 128)
                    S.dma('io', ck_, C['c_rck'][:, t, :])
                    S.dma('io', sk_, C['c_rsk'][:, t, :])
                    proj_tok(P[4][:, :], wn1, t, 0, 512)
                    proj_tok(P[5][:, 0:256], wn2, t, 0, 256)
                    S.tt('dve', a1, P[4][:, 0:128], ck_, ALU.mult)
                    S.tt('dve', a2, P[4][:, 128:256], sk_, ALU.mult)
                    S.tt('pool', kz, a1, a2, ALU.add)
                    S.cp('act', vtok, P[4][:, 256:512])
                    S.act(sg, P[5][:, 0:256], AF.Silu)
                    for h in range(4):
                        hs = slice(32 * h, 32 * h + 32)
                        S.mm(P[h][:, 0:128], kT[hs, tk], qT[hs, tk], tile_position=(32 * h, 0))
                    S.tt('dve', sTm.rearrange("p (h q) -> p h q", h=4), PP[:, 0:4, 0:128],
                         rmask.rearrange("p (h q) -> p h q", h=4), ALU.mult)
                    for h in range(4):
                        hs = slice(32 * h, 32 * h + 32)
                        S.mm(P[6][:, 64 * h:64 * h + 64], sTm[:, h * 128:(h + 1) * 128], vtok[:, 64 * h:64 * h + 64],
                             start=True, stop=False)
                        S.mm(P[6][:, 64 * h:64 * h + 64], qxT[hs, tk], Rbf[hs, t * 64:(t + 1) * 64],
                             start=False, stop=True, tile_position=(32 * h, 0))
                    S.mm(P[5][:, 256:512], kz, vtok)
                    S.tt('dve', tmpu.rearrange("p (h e) -> p h e", h=4), P[5][:, 256:512].rearrange("p (h e) -> p h e", h=4),
                         PRM[:, PC_BD4:PC_BD4 + 4].unsqueeze(2).to_broadcast([128, 4, 64]), ALU.mult)
                    S.red('dve', upd, tmpu.rearrange("p (h e) -> p e h", h=4), ALU.add)
                    S.stt('dve', R32[:, (t + 1) * 64:(t + 2) * 64], R32[:, t * 64:(t + 1) * 64],
                          PRM[:, PC_RDEC:PC_RDEC + 1], upd, ALU.mult, ALU.add)
                    S.cp('pool', Rbf[:, (t + 1) * 64:(t + 2) * 64], R32[:, (t + 1) * 64:(t + 2) * 64])
                    headnorm_gate(P[6][:, 0:256].rearrange("p (h e) -> p h e", h=4), sg, 0, t, hn_wk)

                chk('ret')
                A.reset()
                q0T = A.b16(SEQ)
                q1T = A.b16(SEQ)
                gmask = A.b16(SEQ)
                kdtok = A.b16(16 * 128)
                Sbf = A.b16(33 * 64)
                lT = A.f32(SEQ)
                cumL = A.f32(SEQ)
                gT = A.f32(32)
                S32 = A.f32(128)
                alr = A.b16(512)
                e32 = A.f32(512)
                kdT = A.b16(512)
                S.dma('io', gmask, C['c_gmask'])
                load_w(slot(0), wci[8])
                load_w(slot(1), wci[9])
                load_w(slot(6), wci[10])
                wn4 = slot(2, 4).rearrange("p (k c) -> p k c", k=8)
                load_w(wn4, wn_in[l][:, :, 1024:1536])
                S.memset('pool', q0T, 0.0)
                S.memset('pool', q1T, 0.0)
                S.memset('pool', S32[:, 0:64], 0.0)
                for b in range(4):
                    bs = slice(b * 512, (b + 1) * 512)
                    proj_T(P[0][0:16, :], slot(6), b, M=16)
                    S.cp('act', alr[0:16, :], P[0][0:16, :])
                    S.mm(P[1][:, :], GW[0:16, l * 128:(l + 1) * 128], alr[0:16, :])
                    S.act(e32, P[1][:, :], AF.Exp, bias=PRM[:, PC_NBA + l:PC_NBA + l + 1], scale=-1.0)
                    S.act(lT[:, bs], e32, AF.Ln, bias=1.0)
                S.scan(cumL, gmask, lT, 0.0, ALU.mult, ALU.add)
                c3 = cumL.rearrange("p (c j) -> p c j", j=64)
                S.act(gT, c3[:, :, 63], AF.Exp, scale=-1.0 / 16.0)
                S.tt('dve', lT.rearrange("p (c j) -> p c j", j=64), c3, c3[:, :, 63:64].to_broadcast([128, 32, 64]), ALU.subtract)
                S.act(lT, lT, AF.Exp, scale=1.0 / 16.0)
                for b in range(4):
                    bs = slice(b * 512, (b + 1) * 512)
                    proj_T(P[2], slot(1), b)
                    S.tt('dve', kdT, P[2][:, :], lT[:, bs], ALU.mult)
                    for tt_ in range(4):
                        S.tr(PB[:, tt_ * 128:(tt_ + 1) * 128], kdT[:, tt_ * 128:(tt_ + 1) * 128], ident)
                    S.cp('act', kdtok[:, b * 512:(b + 1) * 512], PB[:, 0:512])
                    proj_T(P[3], slot(0), b)
                    p4 = P[3][:, :].rearrange("p (c w j) -> p c w j", c=4, w=2)
                    S.act(q0T[:, bs].rearrange("p (c w j) -> p c w j", c=4, w=2)[:, :, 0, :], p4[:, :, 0, :], AF.Identity, scale=32.0 ** -0.5)
                    S.act(q1T[:, bs].rearrange("p (c w j) -> p c w j", c=4, w=2)[:, :, 1, :], p4[:, :, 1, :], AF.Identity, scale=32.0 ** -0.5)
                vtok = A.b16(256)
                sg = A.f32(256)
                tmpu = A.f32(256)
                upd = A.f32(64)
                hn_wk = (A.f32(256), A.f32(32), A.f32(256), A.b16(256))
                for t in range(16):
                    tk = slice(t * 128, (t + 1) * 128)
                    proj_tok(P[4][:, :], wn4, t, 0, 512)
                    S.cp('act', vtok, P[4][:, 0:256])
                    S.act(sg, P[4][:, 256:512], AF.Silu)
                    for c2 in range(2):
                        c = 2 * t + c2
                        ps_ = slice(64 * c2, 64 * c2 + 64)
                        pu = P[5 + c2]
                        S.mm(pu[:, 0:256], kdtok[ps_, t * 128:(t + 1) * 128], vtok[ps_, :])
                        S.tt('dve', tmpu.rearrange("p (h e) -> p h e", h=4), pu[:, 0:256].rearrange("p (h e) -> p h e", h=4),
                             PRM[:, PC_BD4:PC_BD4 + 4].unsqueeze(2).to_broadcast([128, 4, 64]), ALU.mult)
                        S.red('dve', upd, tmpu.rearrange("p (h e) -> p e h", h=4), ALU.add)
                        cur = S32[:, (c % 2) * 64:(c % 2) * 64 + 64]
                        nxt = S32[:, ((c + 1) % 2) * 64:((c + 1) % 2) * 64 + 64]
                        S.stt('dve', nxt, cur, gT[:, c:c + 1], upd, ALU.mult, ALU.add)
                        S.cp('pool', Sbf[:, (c + 1) * 64:(c + 2) * 64], nxt)
                    for h in range(4):
                        hs = slice(32 * h, 32 * h + 32)
                        S.mm(P[h][:, 0:64], q0T[hs, tk], Sbf[hs, (2 * t + 1) * 64:(2 * t + 2) * 64],
                             start=True, stop=False, tile_position=(32 * h, 0))
                        S.mm(P[h][:, 0:64], q1T[hs, tk], Sbf[hs, (2 * t + 2) * 64:(2 * t + 3) * 64],
                             start=False, stop=True, tile_position=(32 * h, 0))
                    headnorm_gate(PP[:, 0:4, 0:64], sg, 4, t, hn_wk)

                chk('gla')
                A.reset()
                kTa = A.b16(2 * SEQ)
                Vaug = A.b16(16 * 4 * 66)
                BT = A.b16(2560)
                qTb = A.b16(2 * 512)
                PTs = [A.b16(512) for _ in range(3)]
                rc4 = A.f32(4)
                oat = A.b16(256)
                kTa3 = kTa.rearrange("p (c n) -> p c n", c=2)
                Va4 = Vaug.rearrange("p (t h e) -> p t h e", t=16, h=4)
                BT4 = BT.rearrange("p (t h q) -> p t h q", t=5, h=4)
                qTb3 = qTb.rearrange("p (c n) -> p c n", c=2)
                S.dma('io', BT, att_BT[l])
                for i_ in range(4):
                    load_w(slot(i_), wci[4 + i_])
                wn3 = slot(4, 2).rearrange("p (k c) -> p k c", k=8)
                load_w(wn3, wn_in[l][:, :, 768:1024])
                S.memset('pool', Va4[:, :, :, 64:65], 1.0)
                pO = P[5][:, 0:260].rearrange("p (h e) -> p h e", h=4)
                pti = 0
                for b in range(4):
                    bs = slice(b * 512, (b + 1) * 512)
                    for hp in range(2):
                        proj_T(P[hp], slot(2 + hp), b)
                        S.cp('dve' if hp else 'act', kTa3[:, hp, bs], P[hp][:, :])
                        proj_T(P[2 + hp], slot(hp), b)
                        S.act(qTb3[:, hp, :], P[2 + hp][:, :], AF.Identity, scale=0.125)
                    for tt_ in range(4):
                        t = 4 * b + tt_
                        proj_tok(P[4][:, 0:256], wn3, t, 0, 256)
                        S.cp('act', Va4[:, t, :, 0:64], P[4][:, 0:256].rearrange("p (h e) -> p h e", h=4))
                    for tt_ in range(4):
                        j = 4 * b + tt_
                        valid = [t for t in range(5) if j - 4 + t >= 0]
                        for idx, t in enumerate(valid):
                            kt = j - 4 + t
                            pS = P[6 + (pti % 2)] if False else P[6 if pti % 2 == 0 else 3]
                            for h in range(4):
                                hs = slice(64 * (h % 2), 64 * (h % 2) + 64)
                                S.mm(pS[:, h * 128:(h + 1) * 128], kTa3[hs, h // 2, kt * 128:(kt + 1) * 128],
                                     qTb3[hs, h // 2, tt_ * 128:(tt_ + 1) * 128], start=True, stop=False)
                                S.mm(pS[:, h * 128:(h + 1) * 128], anti, BT4[:, 4 - t, h, :], start=False, stop=True)
                            PTt = PTs[pti % 3]
                            pti += 1
                            S.act(PTt, pS[:, :], AF.Exp)
                            for h in range(4):
                                S.mm(pO[:, h, :], PTt[:, h * 128:(h + 1) * 128], Va4[:, kt, h, 0:65],
                                     start=(idx == 0 and h == 0), stop=(idx == len(valid) - 1), skip_group_check=True)
                        S.recip(rc4, pO[:, :, 64])
                        S.tt('dve', oat.rearrange("p (h e) -> p h e", h=4), pO[:, :, 0:64],
                             rc4.unsqueeze(2).to_broadcast([128, 4, 64]), ALU.mult)
                        for hf in range(2):
                            S.tr(PB[:, hf * 128:(hf + 1) * 128], oat[:, hf * 128:(hf + 1) * 128], ident)
                        S.cp('act', BR[:, 2:4, j * 128:(j + 1) * 128], PB[:, 0:256].rearrange("p (c q) -> p c q", c=2))

                chk('att')
                A.reset()
                uT = A.b16(2 * SEQ)
                uT3 = uT.rearrange("p (c n) -> p c n", c=2)
                diagD = A.b16(256)
                xre = A.b16(512)
                xim = A.b16(512)
                gb = A.b16(1024)
                tabs = [A.f32(1024) for _ in range(2)]
                tq = [A.f32(512) for _ in range(4)]
                btr = A.f32(512)
                bti = A.f32(512)
                xtr = A.f32(512)
                xti = A.f32(512)
                g32 = A.f32(1024)
                ge1 = A.f32(512)
                ge2 = A.f32(512)
                CAR = A.f32(16)
                load_w(slot(0), wci[11])
                load_w(slot(1), wci[12])
                load_w(slot(2, 4), s5_lhs[l])
                wglu = slot(6)[:, 0:512].rearrange("p (c k q) -> p c k q", c=2, k=2)
                load_w(slot(6)[:, 0:512].rearrange("p (c n) -> p c n", c=2), wc_glu[l].rearrange("c p n -> p c n"))
                lhs5 = slot(2, 4).rearrange("p (k j c) -> p k j c", k=8, j=4)
                for hf in range(2):
                    S.ts('dve', diagD[:, hf * 128:(hf + 1) * 128], ident, PRM[:, PC_SD + 2 * l + hf:PC_SD + 2 * l + hf + 1], None, ALU.mult)
                S.memset('pool', CAR, 0.0)
                tbi = 0
                for b in range(4):
                    bs = slice(b * 512, (b + 1) * 512)
                    for hf in range(2):
                        proj_T(P[hf], slot(hf), b)
                        S.cp('act' if hf else 'dve', uT3[:, hf, bs], P[hf][:, :])
                    for ft in range(2):
                        pY = P[2 + ft]
                        S.mm(pY[:, :], diagD[:, ft * 128:(ft + 1) * 128], uT3[:, ft, bs], start=True, stop=False)
                        for k in range(4 * ft, 4 * ft + 4):
                            tab = tabs[tbi % 2]
                            tbi += 1
                            S.dma('io', tab.rearrange("p (a n) -> p a n", a=2), s5_tab[l, k, :, :, bs].rearrange("a p n -> p a n"))
                            cs_ = tab[:, 0:512]
                            sn_ = tab[:, 512:1024]
                            S.mm(P[4][:, :], lhs5[:, k, 0, :], uT3[:, ft, bs])
                            S.mm(P[5][:, :], lhs5[:, k, 1, :], uT3[:, ft, bs])
                            S.tt('dve', tq[0], P[4][:, :], cs_, ALU.mult)
                            S.tt('dve', tq[1], P[5][:, :], sn_, ALU.mult)
                            S.tt('dve', tq[2], P[5][:, :], cs_, ALU.mult)
                            S.tt('dve', tq[3], P[4][:, :], sn_, ALU.mult)
                            S.tt('pool', btr, tq[0], tq[1], ALU.add)
                            S.tt('pool', bti, tq[2], tq[3], ALU.subtract)
                            rho_b = PRM[:, PC_RHO + l * 8 + k:PC_RHO + l * 8 + k + 1].to_broadcast([128, 512])
                            S.scan(xtr, rho_b, btr, CAR[:, 2 * k:2 * k + 1] if b > 0 else 0.0, ALU.mult, ALU.add)
                            S.scan(xti, rho_b, bti, CAR[:, 2 * k + 1:2 * k + 2] if b > 0 else 0.0, ALU.mult, ALU.add)
                            S.cp('act', CAR[:, 2 * k:2 * k + 1], xtr[:, 511:512])
                            S.cp('act', CAR[:, 2 * k + 1:2 * k + 2], xti[:, 511:512])
                            S.tt('pool', tq[0], xtr, cs_, ALU.mult)
                            S.tt('pool', tq[1], xti, sn_, ALU.mult)
                            S.tt('dve', xre, tq[0], tq[1], ALU.subtract)
                            S.tt('pool', tq[2], xtr, sn_, ALU.mult)
                            S.tt('pool', tq[3], xti, cs_, ALU.mult)
                            S.tt('dve', xim, tq[2], tq[3], ALU.add)
                            S.mm(pY[:, :], lhs5[:, k, 2, :], xre, start=False, stop=False)
                            S.mm(pY[:, :], lhs5[:, k, 3, :], xim, start=False, stop=(k == 4 * ft + 3))
                        S.act(ge1, pY[:, :], AF.Square)
                        S.ts('dve', ge1, ge1, 0.044715, 1.0, ALU.mult, ALU.add)
                        S.tt('dve', ge1, ge1, pY[:, :], ALU.mult)
                        S.act(ge2, ge1, AF.Sigmoid, scale=1.5957691216057308)
                        S.tt('dve', g32[:, ft * 512:(ft + 1) * 512], ge2, pY[:, :], ALU.mult)
                        S.cp('pool', gb[:, ft * 512:(ft + 1) * 512], g32[:, ft * 512:(ft + 1) * 512])
                    for ct in range(2):
                        pz = P[6 + ct] if ct == 0 else P[1]
                        for kk in range(2):
                            S.mm(pz[:, :], wglu[:, ct, kk, :], gb[:, kk * 512:(kk + 1) * 512], start=(kk == 0), stop=(kk == 1))
                        S.act(ge2, pz[:, :], AF.Sigmoid, bias=PRM[:, PC_BGLU + 2 * l + ct:PC_BGLU + 2 * l + ct + 1])
                        S.tt('dve', BR[:, 6 + ct, bs], ge2, g32[:, ct * 512:(ct + 1) * 512], ALU.mult)

                chk('s5')
                if dbg == 'br' and s == 0 and l == NL - 1:
                    for r in range(8):
                        dt_ = SCR[:, 0:SEQ]
                        S.cp('dve', dt_, BR[:, r, :])
                        S.dma('io', DBG[:, r, :], dt_)

                A.reset()
                mixT = A.b16(8 * 512)
                hidT = A.b16(NFF * 512)
                brw = [A.b16(1024) for _ in range(2)]
                xb = A.b16(1024)
                sig = [A.f32(512) for _ in range(2)]
                mm_ = [A.f32(512) for _ in range(2)]
                acc = A.f32(512)
                gam = A.f32(1024)
                bet = A.f32(1024)
                lnst = [A.f32(32) for _ in range(4)]
                mix3 = mixT.rearrange("p (c n) -> p c n", c=8)
                hid3 = hidT.rearrange("p (f n) -> p f n", f=NFF)
                last = (l == NL - 1)
                wslot = 0
                for b in range(4):
                    bs = slice(b * 512, (b + 1) * 512)
                    gi = 0
                    for ct in range(8):
                        g0 = (ct % 2) * 4
                        for br_ in range(4):
                            load_w(slot(g0 + br_), wc_gate[l, br_, ct])
                        bw = brw[ct % 2]
                        load_w(bw.rearrange("p (b n) -> p b n", b=4), wc_br[l, :, ct].rearrange("b p n -> p b n"))
                        bw4 = bw.rearrange("p (b k c) -> p b k c", b=4, k=2)
                        for br_ in range(4):
                            pg = P[(2 * br_) % 8]
                            pp = P[(2 * br_ + 1) % 8]
                            proj_T(pg, slot(g0 + br_), b)
                            for kk in range(2):
                                S.mm(pp[:, :], bw4[:, br_, kk, :], BR[:, 2 * br_ + kk, bs], start=(kk == 0), stop=(kk == 1))
                            sg_ = sig[gi % 2]
                            S.act(sg_, pg[:, :], AF.Sigmoid,
                                  bias=PRM[:, PC_BG + l * 32 + br_ * 8 + ct:PC_BG + l * 32 + br_ * 8 + ct + 1])
                            if br_ == 0:
                                S.tt('dve', acc, pp[:, :], sg_, ALU.mult)
                            else:
                                m_ = mm_[gi % 2]
                                S.tt('dve', m_, pp[:, :], sg_, ALU.mult)
                                if br_ < 3:
                                    S.tt('pool', acc, acc, m_, ALU.add)
                                else:
                                    S.tt('pool', mix3[:, ct, :], acc, m_, ALU.add)
                            gi += 1
                    for ct in range(8):
                        ws = slot(ct % 4 + 4 * 0) if False else slot(ct % 8)
                        load_w(ws, wn_out[l, ct])
                        for tt_ in range(4):
                            for hf in range(2):
                                S.mm(P[2 * tt_ + hf][:, :], mix3[:, ct, tt_ * 128:(tt_ + 1) * 128], ws[:, hf * 512:(hf + 1) * 512],
                                     start=(ct == 0), stop=(ct == 7))
                    S.dma('io', gam, bass.AP(I['ln1_g'].tensor, l * D, [[0, 128], [1, D]]))
                    S.dma('io', bet, bass.AP(I['ln1_b'].tensor, l * D, [[0, 128], [1, D]]))
                    layer_norm_multi([(4 * b + tt_, (P[2 * tt_], P[2 * tt_ + 1]), lnst[tt_]) for tt_ in range(4)], gam, bet)
                    for tt_ in range(4):
                        make_xt(4 * b + tt_, xb)
                    for f in range(NFF):
                        s0 = (f % 4) * 2
                        load_w(slot(s0), wc_fg[l, f])
                        load_w(slot(s0 + 1), wc_fu[l, f])
                        pg = P[(2 * f) % 8]
                        pu = P[(2 * f + 1) % 8]
                        proj_T(pg, slot(s0), b)
                        proj_T(pu, slot(s0 + 1), b)
                        sg_ = sig[f % 2]
                        S.act(sg_, pg[:, :], AF.Silu)
                        S.tt('dve', hid3[:, f, :], pu[:, :], sg_, ALU.mult)
                    for f in range(NFF):
                        ws = slot(f % 8)
                        load_w(ws, wn_fd[l, f])
                        for tt_ in range(4):
                            for hf in range(2):
                                S.mm(P[2 * tt_ + hf][:, :], hid3[:, f, tt_ * 128:(tt_ + 1) * 128], ws[:, hf * 512:(hf + 1) * 512],
                                     start=(f == 0), stop=(f == NFF - 1))
                    S.dma('io', gam, bass.AP(I['ln2_g'].tensor, l * D, [[0, 128], [1, D]]))
                    S.dma('io', bet, bass.AP(I['ln2_b'].tensor, l * D, [[0, 128], [1, D]]))
                    layer_norm_multi([(4 * b + tt_, (P[2 * tt_], P[2 * tt_ + 1]), lnst[tt_]) for tt_ in range(4)], gam, bet)
                    for tt_ in range(4):
                        t = 4 * b + tt_
                        if last:
                            S.dma('io', Y[s * SEQ + t * 128:s * SEQ + (t + 1) * 128, :], X32[:, t, :])
                    if not last:
                        for tt_ in range(4):
                            make_xt(4 * b + tt_, xb)
        S.wait_all('sp', ['io', 'w', 'pre'])
    S.close()
    return nc, S


_CACHE = {}


def kernel(**inputs):
    NCORE = 8
    x = np.ascontiguousarray(inputs['x'], dtype=np.float32)
    B = x.shape[0]
    per = B // NCORE
    if 'nc' not in _CACHE:
        _CACHE['nc'] = build(NSEQ=per, NL=DEPTH)[0]
        _CACHE['consts'] = host_consts()
    nc = _CACHE['nc']
    consts = _CACHE['consts']
    shared = {name: np.ascontiguousarray(inputs[name], dtype=np.float32) for name, _ in IN_SPECS}
    in_maps = []
    for c in range(NCORE):
        m = dict(shared)
        m.update(consts)
        m['x'] = x[c * per:(c + 1) * per].reshape(per * SEQ, D)
        in_maps.append(m)
    res = run_bass_kernel_spmd(nc, in_maps, core_ids=list(range(NCORE)))
    outs = [np.asarray(r['y'], dtype=np.float32).reshape(per, SEQ, D) for r in res.results]
    return np.concatenate(outs, axis=0)
```
